# Optimizing a Trainium2 kernel written in Bass

```python
import jax, jax.numpy as jnp
from jax import lax
import numpy as np

D_MODEL = 1024
BATCH = 2
SEQ = 8192
DEPTH = 4

CHUNK = 64
EPS = 1e-5
CONV_WIDTH = D_MODEL
CONV_GROUPS = 16
SHORT_K = 3
SSD_HEAD_DIM = 64
SSD_HEADS = D_MODEL // SSD_HEAD_DIM
SSD_INNER = SSD_HEADS * SSD_HEAD_DIM
SSD_GROUPS = 2
SSD_STATE = 128
SSD_CONV_K = 4
SSD_CONV_DIM = SSD_INNER + 2 * SSD_GROUPS * SSD_STATE
MIX_WIDTH = CONV_WIDTH + SSD_INNER
D_FF = 4 * D_MODEL
IN_COLS = 3 * CONV_WIDTH + SSD_INNER + SSD_CONV_DIM + SSD_HEADS

kernel_name = "hybrid_shortconv_ssd_trunk"


def rmsnorm(x, w):
    xf = x.astype(jnp.float32)
    y = xf * lax.rsqrt(jnp.mean(xf * xf, axis=-1, keepdims=True) + EPS)
    return (y * w.astype(jnp.float32)).astype(x.dtype)


def causal_dwconv(u, w, b=None):
    k, c = w.shape
    y = lax.conv_general_dilated(
        u, w[:, None, :], window_strides=(1,), padding=[(k - 1, 0)],
        dimension_numbers=("NWC", "WIO", "NWC"), feature_group_count=c)
    if b is not None:
        y = y + b
    return y


def short_conv_mixer(u_b, u_c, u_h, conv_w):
    return u_b * causal_dwconv(u_c * u_h, conv_w)


def ssd_scan(xs, dt, a_head, bm, cm):
    f32 = jnp.float32
    b, t, h, p = xs.shape
    g, n = bm.shape[2], bm.shape[3]
    r = h // g
    nc = t // CHUNK
    x_c = (xs.astype(f32) * dt[..., None]).reshape(b, nc, CHUNK, g, r, p)
    a_c = (dt * a_head).reshape(b, nc, CHUNK, g, r)
    b_c = bm.astype(f32).reshape(b, nc, CHUNK, g, n)
    c_c = cm.astype(f32).reshape(b, nc, CHUNK, g, n)
    a_cum = jnp.cumsum(a_c, axis=2)
    causal = jnp.tril(jnp.ones((CHUNK, CHUNK), dtype=bool))
    seg = a_cum[:, :, :, None] - a_cum[:, :, None, :]
    decay = jnp.exp(jnp.where(causal[None, None, :, :, None, None], seg, -jnp.inf))
    scores = jnp.einsum("bclgn,bcsgn->bclsg", c_c, b_c)
    y_diag = jnp.einsum("bclsgr,bcsgrp->bclgrp", scores[..., None] * decay, x_c)
    decay_end = jnp.exp(a_cum[:, :, -1:] - a_cum)
    states = jnp.einsum("bclgn,bclgrp->bcgrpn", b_c, x_c * decay_end[..., None])
    chunk_decay = jnp.exp(a_cum[:, :, -1])

    def step(hs, inp):
        s_c, d_c = inp
        return hs * d_c[..., None, None] + s_c, hs

    h0 = jnp.zeros((b, g, r, p, n), dtype=f32)
    _, prev = lax.scan(step, h0, (jnp.moveaxis(states, 1, 0), jnp.moveaxis(chunk_decay, 1, 0)))
    prev = jnp.moveaxis(prev, 0, 1)
    y_off = jnp.einsum("bclgn,bcgrpn->bclgrp", c_c, prev) * jnp.exp(a_cum)[..., None]
    return (y_diag + y_off).reshape(b, t, h, p)


def ssd_mixer(z, xbc, dt_raw, conv_w, conv_b, dt_bias, a_log, d_skip, norm_w):
    b, t, _ = z.shape
    xbc = jax.nn.silu(causal_dwconv(xbc, conv_w, conv_b))
    xs, bm, cm = jnp.split(xbc, [SSD_INNER, SSD_INNER + SSD_GROUPS * SSD_STATE], axis=-1)
    xs = xs.reshape(b, t, SSD_HEADS, SSD_HEAD_DIM)
    bm = bm.reshape(b, t, SSD_GROUPS, SSD_STATE)
    cm = cm.reshape(b, t, SSD_GROUPS, SSD_STATE)
    dt = jax.nn.softplus(dt_raw.astype(jnp.float32) + dt_bias.astype(jnp.float32))
    a_head = -jnp.exp(a_log.astype(jnp.float32))
    y = ssd_scan(xs, dt, a_head, bm, cm)
    y = y + d_skip.astype(jnp.float32)[:, None] * xs.astype(jnp.float32)
    y = y.reshape(b, t, SSD_INNER).astype(z.dtype)
    gated = (y * jax.nn.silu(z)).reshape(b, t, SSD_GROUPS, SSD_INNER // SSD_GROUPS)
    gated = rmsnorm(gated, norm_w.reshape(SSD_GROUPS, SSD_INNER // SSD_GROUPS))
    return gated.reshape(b, t, SSD_INNER)


SPLITS = list(np.cumsum([CONV_WIDTH, CONV_WIDTH, CONV_WIDTH, SSD_INNER, SSD_CONV_DIM]))


def hybrid_layer(x, norm_mix_w, w_in, short_conv_w, ssd_conv_w, ssd_conv_b, dt_bias,
                 a_log, d_skip, ssd_norm_w, w_out, norm_mlp_w, w_up, w_down):
    h = rmsnorm(x, norm_mix_w)
    proj = jnp.einsum("btd,dc->btc", h, w_in)
    u_b, u_c, u_h, z, xbc, dt_raw = jnp.split(proj, SPLITS, axis=-1)
    y_a = short_conv_mixer(u_b, u_c, u_h, short_conv_w)
    y_b = ssd_mixer(z, xbc, dt_raw, ssd_conv_w, ssd_conv_b, dt_bias, a_log, d_skip, ssd_norm_w)
    y = jnp.concatenate([y_a, y_b], axis=-1)
    x = x + jnp.einsum("btc,cd->btd", y, w_out)
    h = rmsnorm(x, norm_mlp_w)
    hid = jnp.square(jax.nn.relu(jnp.einsum("btd,df->btf", h, w_up)))
    return x + jnp.einsum("btf,fd->btd", hid, w_down)


def setup_inputs(seed: int = 0) -> dict:
    key = jax.random.key(seed)
    ks = jax.random.split(key, 16)
    f32 = jnp.float32
    nrm = lambda k, shape, s: jax.random.normal(k, shape, f32) * s
    gain = lambda k, shape: 1.0 + 0.02 * jax.random.normal(k, shape, f32)
    dt0 = jnp.exp(jax.random.uniform(ks[6], (DEPTH, SSD_HEADS), f32, math_log(1e-3), math_log(1e-1)))
    dt_bias = dt0 + jnp.log(-jnp.expm1(-dt0))
    return {
        "x": jax.random.normal(ks[0], (BATCH, SEQ, D_MODEL), f32),
        "norm_mix_w": gain(ks[1], (DEPTH, D_MODEL)),
        "w_in": nrm(ks[2], (DEPTH, D_MODEL, IN_COLS), D_MODEL ** -0.5),
        "short_conv_w": nrm(ks[3], (DEPTH, SHORT_K, CONV_WIDTH), SHORT_K ** -0.5),
        "ssd_conv_w": nrm(ks[4], (DEPTH, SSD_CONV_K, SSD_CONV_DIM), SSD_CONV_K ** -0.5),
        "ssd_conv_b": nrm(ks[5], (DEPTH, SSD_CONV_DIM), 0.02),
        "dt_bias": dt_bias,
        "a_log": jnp.log(jax.random.uniform(ks[7], (DEPTH, SSD_HEADS), f32, 1.0, 16.0)),
        "d_skip": gain(ks[8], (DEPTH, SSD_HEADS)),
        "ssd_norm_w": gain(ks[9], (DEPTH, SSD_INNER)),
        "w_out": nrm(ks[10], (DEPTH, MIX_WIDTH, D_MODEL), MIX_WIDTH ** -0.5),
        "norm_mlp_w": gain(ks[11], (DEPTH, D_MODEL)),
        "w_up": nrm(ks[12], (DEPTH, D_MODEL, D_FF), D_MODEL ** -0.5),
        "w_down": nrm(ks[13], (DEPTH, D_FF, D_MODEL), D_FF ** -0.5),
        "final_norm_w": gain(ks[14], (D_MODEL,)),
    }


def math_log(v):
    return float(np.log(v))


def reference(x, norm_mix_w, w_in, short_conv_w, ssd_conv_w, ssd_conv_b, dt_bias, a_log,
              d_skip, ssd_norm_w, w_out, norm_mlp_w, w_up, w_down, final_norm_w):
    for i in range(DEPTH):
        x = hybrid_layer(x, norm_mix_w[i], w_in[i], short_conv_w[i], ssd_conv_w[i],
                         ssd_conv_b[i], dt_bias[i], a_log[i], d_skip[i], ssd_norm_w[i],
                         w_out[i], norm_mlp_w[i], w_up[i], w_down[i])
    return rmsnorm(x, final_norm_w)
```

```python
import numpy as np
from contextlib import ExitStack
import concourse.bass as bass
import concourse.mybir as mybir
from concourse.bass_utils import run_bass_kernel_spmd

F32 = mybir.dt.float32
F32R = mybir.dt.float32r
AF = mybir.ActivationFunctionType
ALU = mybir.AluOpType

DEPTH = 4
D = 1024
NT = 2048
T = 256
NSUB = T // 128
NTILE = NT // T
HALO = 4
EPS = 1e-5
NPAN = 34
PAN_OUT = 14
PAN_UP = 18
PAN_DN = 26
NWBUF = 2


class _Rec:
    def __getattr__(self, name):
        def f(*a, **kw):
            return (name, a, kw)
        return f


R = _Rec()


class Prog:
    ENG = ["tensor", "vector", "scalar", "gpsimd", "sync"]

    def __init__(self, nc, st, ndma=16):
        self.nc = nc
        self.lists = {e: [] for e in self.ENG}
        self.count = {e: 0 for e in self.ENG}
        self.sem = {}
        self.seen = {e: {} for e in self.ENG}
        self.last_w = {}
        self.reads = {}
        self.last_acc = {}
        self.dma_cnt = []
        self.dma_rr = 0
        for e in self.ENG:
            self.sem[e] = st.enter_context(nc.semaphore("s_" + e))
        for j in range(ndma):
            self.sem[("dma", j)] = st.enter_context(nc.semaphore("dq%d" % j))
            self.dma_cnt.append(0)

    def token_wait(self, eng, tok, force=False):
        key, val = tok
        if key == eng and eng == "tensor" and not force:
            return
        cur = self.seen[eng].get(key, 0)
        if cur >= val:
            return
        self.seen[eng][key] = val
        sem = self.sem[key]
        self.lists[eng].append(lambda e, sem=sem, val=val: e.wait_ge(sem, val))

    def _deps(self, eng, reads, writes, extra, newgroup=False):
        deps = list(extra)
        if newgroup and eng == "tensor":
            for k in writes:
                lw = self.last_w.get(k)
                if lw is not None and lw[0] == "tensor" and not self.reads.get(k):
                    self.token_wait(eng, lw, force=True)
        for k in reads:
            if k in self.last_w:
                deps.append(self.last_w[k])
        for k in writes:
            if k in self.last_w:
                deps.append(self.last_w[k])
            deps.extend(self.reads.get(k, ()))
        for k in list(reads) + list(writes):
            if isinstance(k, tuple) and k[0] == "ps":
                for oe, t in self.last_acc.get(k, {}).items():
                    if oe != eng:
                        deps.append(t)
        for t in deps:
            self.token_wait(eng, t)

    def _commit(self, tok, reads, writes):
        for k in list(reads) + list(writes):
            if isinstance(k, tuple) and k[0] == "ps":
                self.last_acc.setdefault(k, {})[tok[0]] = tok
        for k in writes:
            self.last_w[k] = tok
            self.reads[k] = []
        for k in reads:
            if k not in writes:
                self.reads.setdefault(k, []).append(tok)

    def op(self, eng, fn, reads=(), writes=(), extra=(), newgroup=False):
        self._deps(eng, reads, writes, extra, newgroup)
        self.count[eng] += 1
        idx = self.count[eng]
        sem = self.sem[eng]
        name, a, kw = fn
        self.lists[eng].append(lambda e, name=name, a=a, kw=kw, sem=sem: getattr(e, name)(*a, **kw).then_inc(sem, 1))
        tok = (eng, idx)
        self._commit(tok, reads, writes)
        return tok

    def dma(self, eng, out, in_, reads=(), writes=(), extra=()):
        self._deps(eng, reads, writes, extra)
        j = self.dma_rr % len(self.dma_cnt)
        self.dma_rr += 1
        key = ("dma", j)
        if self.dma_cnt[j] > 0:
            self.token_wait(eng, (key, self.dma_cnt[j]))
        self.dma_cnt[j] += 16
        val = self.dma_cnt[j]
        sem = self.sem[key]
        self.lists[eng].append(
            lambda e, sem=sem, out=out, in_=in_: e.dma_start(out=out, in_=in_).then_inc(sem, 16))
        tok = (key, val)
        self._commit(tok, reads, writes)
        return tok

    def finish(self, block, final_tokens):
        for t in final_tokens:
            self.token_wait("sync", t)
        for en in self.ENG:
            def body(e, en=en):
                for f in self.lists[en]:
                    f(e)
            getattr(block, en)(body)


def build_program(mode):
    full = mode == "B"
    nc = bass.Bass("TRN2", target_bir_lowering=False)

    def din(name, shape):
        return nc.dram_tensor(name, list(shape), F32, kind="ExternalInput").ap()

    def dout(name, shape):
        return nc.dram_tensor(name, list(shape), F32, kind="ExternalOutput").ap()

    xT_d = din("xT", [D, NT])
    xh_d = din("xh", [D, HALO])
    wall_d = din("wall", [NPAN, 128, 4096])
    pvec_d = din("pvec", [128, 8 * 3 + 8 * 3 + 12 * 4 + 12 + 16 * 3])
    snw_d = din("snw", [128, 1024])
    cst_d = din("cst", [4, 128, 128])
    if full:
        hp_d = din("hp", [3, 128, 1024])
        dp_d = din("dp", [3, 128, 16])
        xo_d = dout("xo", [D, NT])
        yo_d = dout("yo", [D, NT])
    else:
        hloc_d = dout("hloc", [128, 1024])
        dtot_d = dout("dtot", [128, 16])

    with ExitStack() as st:
        E = st.enter_context

        def sb(name, shape, dt=F32):
            return E(nc.sbuf_tensor("sb_" + name, list(shape), dt))

        x_t = sb("x_t", [128, 8, T])
        xh_t = sb("xh_t", [128, 8, HALO])
        h_t = sb("h_t", [128, 8, T], F32R)
        hh_t = sb("hh_t", [128, 8, HALO], F32R)
        sqh_t = sb("sqh_t", [128, 8, HALO], F32R)
        rs_t = sb("rs_t", [128, T])
        ln_t = sb("ln_t", [128, T])
        vbuf = sb("vbuf", [128, 8, HALO + T])
        uc_sb = sb("uc_sb", [128, T])
        acc_a = sb("acc_a", [128, T])
        acc_b = sb("acc_b", [128, T])
        ycat = sb("ycat", [128, 16, T], F32R)
        xpre = sb("xpre", [128, 12, HALO + T])
        xsT = sb("xsT", [128, 8, T])
        bcT = sb("bcT", [128, 4, T], F32R)
        sz = sb("sz", [128, NSUB, 1024])
        dt_tm = sb("dt_tm", [128, NSUB, 16])
        dt1 = sb("dt1", [128, 16])
        a_tm = sb("a_tm", [128, 16])
        acum_sb = sb("acum_sb", [128, 16])
        E_tm = sb("E_tm", [128, 16])
        W_tm = sb("W_tm", [128, 16])
        cd_t = sb("cd_t", [128, 16])
        scr = sb("scr", [128, 4096])
        amask = scr[:, 0:2048]
        Dk = scr[:, 2048:4096]
        hid = sb("hid", [128, 16, T], F32R)
        M_t = sb("M_t", [128, 2048], F32R)
        sq_t = M_t[:].rearrange("p (c t) -> p c t", t=T)
        Gm = sb("Gm", [128, 2, 128])
        xc = sb("xc", [128, 1024], F32R)
        xsD = sb("xsD", [128, 1024], F32R)
        xw = sb("xw", [128, 1024], F32R)
        Btm = sb("Btm", [128, 2, 128], F32R)
        t1 = sb("t1", [128, 1024])
        t2 = sb("t2", [128, 1024])
        gn = sb("gn", [128, 1024])
        junk = sb("junk", [128, 512])
        ssq = sb("ssq", [128, 2])
        rsg = sb("rsg", [128, 2])
        H = sb("H", [128, 1024])
        Hr = sb("Hr", [128, 1024], F32R)
        relu_t = sb("relu_t", [128, T])
        wbuf = sb("wbuf", [128, NWBUF, 4096], F32R)
        cst = sb("cst", [128, 4, 128])
        cst_r = sb("cst_r", [128, 4, 128], F32R)
        pvec = sb("pvec", [128, 156])
        snw = sb("snw", [128, 1024])
        A_bc = sb("A_bc", [128, 16])
        dsum = sb("dsum", [128, 16])
        epsb = sb("epsb", [128, 1])
        oneb = sb("oneb", [128, 1])
        if full:
            dp_t = sb("dp_t", [128, 3, 16])
            yfin = xsT
        pst = E(nc.psum_tensor("pst", [128, 8, 512], F32))

        P = Prog(nc, st)
        block = E(nc.Block())

        tri = cst[:, 0, :]
        sup = cst[:, 1, :]
        ident = cst[:, 3, :]
        ones_r = cst_r[:, 2, :]
        ident_r = cst_r[:, 3, :]
        nmix = pvec[:, 0:8]
        nmlp = pvec[:, 8:16]
        fnw = pvec[:, 16:24]
        scw = pvec[:, 24:48].rearrange("p (c k) -> p c k", k=3)
        ccw = pvec[:, 48:96].rearrange("p (c k) -> p c k", k=4)
        ccb = pvec[:, 96:108]
        dtb = pvec[:, 108:124]
        alog = pvec[:, 124:140]
        dsk = pvec[:, 140:156]

        def PS(b, n=512, p0=0, p1=128, c0=0):
            return pst[p0:p1, b, c0:c0 + n]

        def psk(*bs):
            return [("ps", b) for b in bs]

        P.dma("sync", cst[:], cst_d.rearrange("c p q -> p c q"), writes=["cst"])
        P.dma("gpsimd", cst_r[:], cst_d.rearrange("c p q -> p c q"), writes=["cst_r"])
        P.dma("sync", pvec[:], pvec_d, writes=["pvec"])
        P.dma("sync", snw[:], snw_d, writes=["snw"])
        P.op("vector", R.memset(epsb[:], EPS), writes=["epsb"])
        P.op("vector", R.memset(oneb[:], 1.0), writes=["oneb"])
        P.op("scalar", R.activation(out=A_bc[:], in_=alog, func=AF.Exp), reads=["pvec"], writes=["A_bc"])
        P.op("vector", R.tensor_scalar(out=A_bc[:], in0=A_bc[:], scalar1=-1.0, scalar2=None, op0=ALU.mult),
             reads=["A_bc"], writes=["A_bc"])
        if full:
            hp_l = [t1, t2, gn]
            for j in range(3):
                P.dma("sync", hp_l[j][:], hp_d[j], writes=[["t1", "t2", "gn"][j]])
            P.dma("sync", dp_t[:], dp_d.rearrange("j p q -> p j q"), writes=["dp"])
            P.op("scalar", R.activation(out=dp_t[:], in_=dp_t[:], func=AF.Exp), reads=["dp"], writes=["dp"])
            def bc16(ap):
                return ap.unsqueeze(2).to_broadcast([128, 16, 64])
            def v3(ap):
                return ap.rearrange("p (h q) -> p h q", q=64)
            P.op("vector", R.tensor_tensor(out=v3(H[:]), in0=v3(hp_l[2][:]), in1=bc16(dp_t[:, 1, :]), op=ALU.mult),
                 reads=["gn", "dp"], writes=["H"])
            P.op("vector", R.tensor_tensor(out=H[:], in0=H[:], in1=hp_l[1][:], op=ALU.add),
                 reads=["t2", "H"], writes=["H"])
            P.op("vector", R.tensor_tensor(out=v3(H[:]), in0=v3(H[:]), in1=bc16(dp_t[:, 0, :]), op=ALU.mult),
                 reads=["dp", "H"], writes=["H"])
            P.op("vector", R.tensor_tensor(out=H[:], in0=H[:], in1=hp_l[0][:], op=ALU.add),
                 reads=["t1", "H"], writes=["H"])
        else:
            P.op("vector", R.memset(H[:], 0.0), writes=["H"])
            P.op("vector", R.memset(dsum[:], 0.0), writes=["dsum"])
        P.op("scalar", R.activation(out=Hr[:], in_=H[:], func=AF.Copy), reads=["H"], writes=["Hr"])

        def bc16(ap):
            return ap.unsqueeze(2).to_broadcast([128, 16, 64])

        def v3(ap):
            return ap.rearrange("p (h q) -> p h q", q=64)

        wstate = {"n": 0}

        def load_panel(pn):
            slot = wstate["n"] % NWBUF
            wstate["n"] += 1
            P.dma("gpsimd", wbuf[:, slot, :], wall_d[pn], writes=[("w", slot)])
            return slot

        def wv(slot, kchunks, width):
            return wbuf[:, slot, 0:kchunks * width].rearrange("p (k c) -> p k c", c=width)

        psrr = {"n": 0}

        def next_bank():
            b = psrr["n"] % 4
            psrr["n"] += 1
            return b

        def rmsnorm(x_ap, xkey, n, nw, sq_ap, sqkey, out_ap, outkey, out_dt_r=True):
            P.op("scalar", R.activation(out=sq_ap, in_=x_ap, func=AF.Square),
                 reads=[xkey], writes=[sqkey])
            b = next_bank()
            for c in range(8):
                P.op("tensor", R.matmul(PS(b, n), lhsT=ones_r, rhs=sq_ap[:, c, :],
                                                        start=(c == 0), stop=(c == 7)),
                     reads=[sqkey, "cst_r"], writes=psk(b))
            P.op("scalar", R.activation(out=ln_t[:, 0:n], in_=PS(b, n), func=AF.Ln,
                                                 bias=epsb[:], scale=1.0 / D),
                 reads=psk(b) + ["epsb"], writes=["ln"])
            P.op("scalar", R.activation(out=rs_t[:, 0:n], in_=ln_t[:, 0:n], func=AF.Exp, scale=-0.5),
                 reads=["ln"], writes=["rs"])
            for c in range(8):
                P.op("vector", R.scalar_tensor_tensor(
                    out=out_ap[:, c, :], in0=x_ap[:, c, :], scalar=nw[:, c:c + 1], in1=rs_t[:, 0:n],
                    op0=ALU.mult, op1=ALU.mult),
                     reads=[xkey, "rs", "pvec"], writes=[outkey])

        def fm_chunk(slot, mc, h_ap, hkey, n, width=512):
            b = next_bank()
            w = wv(slot, 8, width)
            for k in range(8):
                P.op("tensor", R.matmul(PS(b, n), lhsT=w[:, k, mc * 128:(mc + 1) * 128],
                                                        rhs=h_ap[:, k, :], start=(k == 0), stop=(k == 7)),
                     reads=[hkey, ("w", slot)], writes=psk(b))
            return b

        def conv_panel(j, first):
            slot = load_panel(j)
            if first:
                b1 = fm_chunk(slot, 1, hh_t, "hh", HALO)
                P.op("scalar", R.activation(out=uc_sb[:, 0:HALO], in_=PS(b1, HALO), func=AF.Copy),
                     reads=psk(b1), writes=["uc"])
                b2 = fm_chunk(slot, 2, hh_t, "hh", HALO)
                P.op("vector", R.tensor_tensor(out=vbuf[:, j, 0:HALO], in0=uc_sb[:, 0:HALO],
                                                         in1=PS(b2, HALO), op=ALU.mult),
                     reads=psk(b2) + ["uc"], writes=[("v", j)])
            b1 = fm_chunk(slot, 1, h_t, "h", T)
            P.op("scalar", R.activation(out=uc_sb[:], in_=PS(b1, T), func=AF.Copy),
                 reads=psk(b1), writes=["uc"])
            b2 = fm_chunk(slot, 2, h_t, "h", T)
            P.op("vector", R.tensor_tensor(out=vbuf[:, j, HALO:HALO + T], in0=uc_sb[:],
                                                     in1=PS(b2, T), op=ALU.mult),
                 reads=psk(b2) + ["uc"], writes=[("v", j)])
            b0 = fm_chunk(slot, 0, h_t, "h", T)
            P.op("vector", R.tensor_scalar(out=acc_a[:], in0=vbuf[:, j, HALO:HALO + T],
                                                     scalar1=scw[:, j, 2:3], scalar2=None, op0=ALU.mult),
                 reads=[("v", j), "pvec"], writes=["acc_a"])
            for kk in (1, 0):
                sh = 2 - kk
                P.op("vector", R.scalar_tensor_tensor(
                    out=acc_a[:], in0=vbuf[:, j, HALO - sh:HALO - sh + T], scalar=scw[:, j, kk:kk + 1],
                    in1=acc_a[:], op0=ALU.mult, op1=ALU.add),
                     reads=[("v", j), "pvec", "acc_a"], writes=["acc_a"])
            P.op("vector", R.tensor_tensor(out=ycat[:, j, :], in0=acc_a[:], in1=PS(b0, T), op=ALU.mult),
                 reads=psk(b0) + ["acc_a"], writes=[("ycat", j)])
            P.op("scalar", R.activation(out=vbuf[:, j, 0:HALO], in_=vbuf[:, j, T:T + HALO], func=AF.Copy),
                 reads=[("v", j)], writes=[("v", j)])

        def xbc_panel(q, first, chunks=(0, 1, 2, 3)):
            slot = load_panel(8 + q)
            for mc in chunks:
                c = q * 4 + mc
                if first:
                    bh = fm_chunk(slot, mc, hh_t, "hh", HALO)
                    P.op("scalar", R.activation(out=xpre[:, c, 0:HALO], in_=PS(bh, HALO),
                                                                       func=AF.Copy),
                         reads=psk(bh), writes=[("xp", c)])
                b = fm_chunk(slot, mc, h_t, "h", T)
                P.op("scalar", R.activation(out=xpre[:, c, HALO:HALO + T], in_=PS(b, T),
                                                                 func=AF.Copy),
                     reads=psk(b), writes=[("xp", c)])
                P.op("vector", R.tensor_scalar(
                    out=acc_b[:], in0=xpre[:, c, HALO:HALO + T], scalar1=ccw[:, c, 3:4], scalar2=ccb[:, c:c + 1],
                    op0=ALU.mult, op1=ALU.add),
                     reads=[("xp", c), "pvec"], writes=["acc_b"])
                for kk in (2, 1, 0):
                    sh = 3 - kk
                    P.op("vector", R.scalar_tensor_tensor(
                        out=acc_b[:], in0=xpre[:, c, HALO - sh:HALO - sh + T], scalar=ccw[:, c, kk:kk + 1],
                        in1=acc_b[:], op0=ALU.mult, op1=ALU.add),
                         reads=[("xp", c), "pvec", "acc_b"], writes=["acc_b"])
                if c < 8:
                    P.op("scalar", R.activation(out=xsT[:, c, :], in_=acc_b[:], func=AF.Silu),
                         reads=["acc_b"], writes=[("xsT", c)])
                else:
                    P.op("scalar", R.activation(out=bcT[:, c - 8, :], in_=acc_b[:], func=AF.Silu),
                         reads=["acc_b"], writes=[("bcT", c - 8)])
                P.op("scalar", R.activation(out=xpre[:, c, 0:HALO], in_=xpre[:, c, T:T + HALO],
                                                           func=AF.Copy),
                     reads=[("xp", c)], writes=[("xp", c)])

        def z_panel(zp):
            slot = load_panel(11 + zp)
            w = wv(slot, 8, 512)
            for s in range(NSUB):
                b = next_bank()
                for k in range(8):
                    P.op("tensor", R.matmul(PS(b), lhsT=h_t[:, k, s * 128:(s + 1) * 128],
                                                                 rhs=w[:, k, :], start=(k == 0), stop=(k == 7)),
                         reads=["h", ("w", slot)], writes=psk(b))
                P.op("scalar", R.activation(out=sz[:, s, zp * 512:(zp + 1) * 512], in_=PS(b),
                                                                 func=AF.Silu),
                     reads=psk(b), writes=[("sz", s)])

        def dt_panel():
            slot = load_panel(13)
            w = wv(slot, 8, 512)
            for s in range(NSUB):
                b = next_bank()
                for k in range(8):
                    P.op("tensor", R.matmul(PS(b, 16), lhsT=h_t[:, k, s * 128:(s + 1) * 128],
                                                                 rhs=w[:, k, 0:16], start=(k == 0), stop=(k == 7)),
                         reads=["h", ("w", slot)], writes=psk(b))
                P.op("vector", R.tensor_tensor(out=dt1[:], in0=PS(b, 16), in1=dtb, op=ALU.add),
                     reads=psk(b) + ["pvec"], writes=["dt1"])
                P.op("scalar", R.activation(out=dt1[:], in_=dt1[:], func=AF.Exp),
                     reads=["dt1"], writes=["dt1"])
                P.op("scalar", R.activation(out=dt_tm[:, s, :], in_=dt1[:], func=AF.Ln,
                                                           bias=oneb[:], scale=1.0),
                     reads=["dt1", "oneb"], writes=[("dt", s)])

        finals = []
        dbgstate = {"n": 0}
        def ssd_chunk(s):
            tok = slice(s * 128, (s + 1) * 128)
            dts = dt_tm[:, s, :]
            P.op("vector", R.tensor_tensor(out=a_tm[:], in0=dts, in1=A_bc[:], op=ALU.mult),
                 reads=[("dt", s), "A_bc"], writes=["a_tm"])
            P.op("tensor", R.matmul(PS(6, 16), lhsT=tri, rhs=a_tm[:], start=True, stop=True),
                 reads=["cst", "a_tm"], writes=psk(6))
            P.op("tensor", R.matmul(PS(6, 16, c0=16), lhsT=cst[:, 2, :], rhs=a_tm[:], start=True, stop=True),
                 reads=["cst", "a_tm"], writes=psk(6), newgroup=True)
            P.op("scalar", R.activation(out=acum_sb[:], in_=PS(6, 16), func=AF.Copy),
                 reads=psk(6), writes=["acum"])
            P.op("scalar", R.activation(out=E_tm[:], in_=PS(6, 16), func=AF.Exp),
                 reads=psk(6), writes=["E_tm"])
            P.op("scalar", R.activation(out=cd_t[:], in_=PS(6, 16, c0=16), func=AF.Exp),
                 reads=psk(6), writes=["cd"])
            P.op("vector", R.tensor_tensor(out=W_tm[:], in0=PS(6, 16, c0=16), in1=acum_sb[:], op=ALU.subtract),
                 reads=psk(6) + ["acum"], writes=["W_tm"])
            P.op("scalar", R.activation(out=W_tm[:], in_=W_tm[:], func=AF.Exp),
                 reads=["W_tm"], writes=["W_tm"])
            if not full:
                P.op("vector", R.tensor_tensor(out=dsum[:], in0=dsum[:], in1=PS(6, 16, c0=16), op=ALU.add),
                     reads=psk(6) + ["dsum"], writes=["dsum"])
            if False:
                gi = dbgstate["n"]
                dbgstate["n"] += 1
                for (i, (ap, key)) in enumerate([(dts, ("dt", s)), (a_tm[:], "a_tm"), (acum_sb[:], "acum"), (W_tm[:], "W_tm"), (cd_t[:], "cd")]):
                    finals.append(P.dma("sync", dbg_d[gi, :, i * 16:(i + 1) * 16], ap, reads=[key]))
            for g in range(2):
                P.op("tensor", R.transpose(PS(7, 128, c0=256 + g * 128), bcT[:, g, tok].bitcast(F32), ident),
                     reads=[("bcT", g), "cst"], writes=psk(7))
            P.op("scalar", R.activation(out=Btm[:].rearrange("p g n -> p (g n)"), in_=PS(7, 256, c0=256),
                                                 func=AF.Copy),
                 reads=psk(7), writes=["Btm"])
            for c in range(8):
                bb = 4 + c // 4
                P.op("tensor", R.transpose(PS(bb, 128, c0=(c % 4) * 128), xsT[:, c, tok], ident),
                     reads=[("xsT", c), "cst"], writes=psk(bb))
            xs_ps = pst[:, 4:6, :].rearrange("p a b -> p (a b)")
            P.op("vector", R.tensor_tensor(out=v3(xc[:]), in0=v3(xs_ps), in1=bc16(dts), op=ALU.mult),
                 reads=psk(4, 5) + [("dt", s)], writes=["xc"])
            if full:
                P.op("vector", R.tensor_tensor(out=v3(xsD[:]), in0=v3(xs_ps), in1=bc16(dsk), op=ALU.mult),
                     reads=psk(4, 5) + ["pvec"], writes=["xsD"])
            if full:
                P.op("vector", R.tensor_tensor(
                    out=amask.rearrange("p (h l) -> p h l", l=128),
                    in0=a_tm[:].unsqueeze(2).to_broadcast([128, 16, 128]),
                    in1=tri.unsqueeze(1).to_broadcast([128, 16, 128]), op=ALU.mult),
                     reads=["a_tm", "cst"], writes=["amask"])
                for q in range(4):
                    P.op("tensor", R.matmul(PS(q), lhsT=sup, rhs=amask[:, q * 512:(q + 1) * 512],
                                                            start=True, stop=True),
                         reads=["amask", "cst"], writes=psk(q))
                for q in range(2):
                    P.op("scalar", R.activation(
                        out=Dk[:, q * 1024:(q + 1) * 1024],
                        in_=pst[:, 2 * q:2 * q + 2, :].rearrange("p a b -> p (a b)"), func=AF.Exp),
                         reads=psk(2 * q, 2 * q + 1), writes=["Dk"])
                for g in range(2):
                    P.op("tensor", R.matmul(PS(7, 128, c0=g * 128), lhsT=bcT[:, g, tok],
                                                            rhs=bcT[:, 2 + g, tok], start=True, stop=True),
                         reads=[("bcT", g), ("bcT", 2 + g)], writes=psk(7))
                P.op("vector", R.tensor_tensor(
                    out=Gm[:], in0=PS(7, 256).rearrange("p (g l) -> p g l", l=128),
                    in1=tri.unsqueeze(1).to_broadcast([128, 2, 128]), op=ALU.mult),
                     reads=psk(7) + ["cst"], writes=["Gm"])
                for g in range(2):
                    P.op("vector", R.tensor_tensor(
                        out=M_t[:, g * 1024:(g + 1) * 1024].rearrange("p (h l) -> p h l", l=128),
                        in0=Dk[:, g * 1024:(g + 1) * 1024].rearrange("p (h l) -> p h l", l=128),
                        in1=Gm[:, g, :].unsqueeze(1).to_broadcast([128, 8, 128]), op=ALU.mult),
                         reads=["Dk", "Gm"], writes=["M"])
                for g in range(2):
                    bb = 4 + g
                    P.op("tensor", R.matmul(PS(bb), lhsT=ident_r, rhs=xsD[:, g * 512:(g + 1) * 512],
                                                                   start=True, stop=False, skip_group_check=True),
                         reads=["xsD", "cst_r"], writes=psk(bb))
                    for hh in range(8):
                        hd = g * 8 + hh
                        P.op("tensor", R.matmul(
                            PS(bb, 64, c0=hh * 64), lhsT=M_t[:, hd * 128:(hd + 1) * 128],
                            rhs=xc[:, hd * 64:(hd + 1) * 64], start=False, stop=(hh == 7), skip_group_check=True),
                             reads=["M", "xc"], writes=psk(bb))
                for g in range(2):
                    P.op("tensor", R.matmul(PS(6 + g), lhsT=bcT[:, 2 + g, tok],
                                                            rhs=Hr[:, g * 512:(g + 1) * 512], start=True, stop=True),
                         reads=[("bcT", 2 + g), "Hr"], writes=psk(6 + g))
                yo_ps = pst[:, 6:8, :].rearrange("p a b -> p (a b)")
                y_ps = pst[:, 4:6, :].rearrange("p a b -> p (a b)")
                P.op("vector", R.tensor_tensor(out=v3(t1[:]), in0=v3(yo_ps), in1=bc16(E_tm[:]), op=ALU.mult),
                     reads=psk(6, 7) + ["E_tm"], writes=["t1"])
                P.op("vector", R.tensor_tensor(out=t2[:], in0=y_ps, in1=t1[:], op=ALU.add),
                     reads=psk(4, 5) + ["t1"], writes=["t2"])
                P.op("vector", R.tensor_tensor(out=t2[:], in0=t2[:], in1=sz[:, s, :], op=ALU.mult),
                     reads=["t2", ("sz", s)], writes=["t2"])
                for g in range(2):
                    P.op("scalar", R.activation(out=junk[:], in_=t2[:, g * 512:(g + 1) * 512],
                                                               func=AF.Square, accum_out=ssq[:, g:g + 1]),
                         reads=["t2"], writes=["junk", ("ssq", g)])
                P.op("scalar", R.activation(out=rsg[:], in_=ssq[:], func=AF.Ln, bias=epsb[:], scale=1.0 / 512),
                     reads=[("ssq", 0), ("ssq", 1), "epsb"], writes=["rsg"])
                P.op("scalar", R.activation(out=rsg[:], in_=rsg[:], func=AF.Exp, scale=-0.5),
                     reads=["rsg"], writes=["rsg"])
                for g in range(2):
                    P.op("vector", R.scalar_tensor_tensor(
                        out=gn[:, g * 512:(g + 1) * 512], in0=t2[:, g * 512:(g + 1) * 512], scalar=rsg[:, g:g + 1],
                        in1=snw[:, g * 512:(g + 1) * 512], op0=ALU.mult, op1=ALU.mult),
                         reads=["t2", "rsg", "snw"], writes=["gn"])
                for c in range(8):
                    bb = c // 4
                    P.op("tensor", R.transpose(PS(bb, 128, c0=(c % 4) * 128),
                                                                     gn[:, c * 128:(c + 1) * 128], ident),
                         reads=["gn", "cst"], writes=psk(bb))
                for bb in range(2):
                    P.op("scalar", R.activation(
                        out=ycat[:, 8 + bb * 4:12 + bb * 4, tok],
                        in_=PS(bb).rearrange("p (c t) -> p c t", t=128), func=AF.Copy),
                         reads=psk(bb), writes=[("ycat", 8 + bb * 4 + i) for i in range(4)])
            P.op("vector", R.tensor_tensor(out=v3(xw[:]), in0=v3(xc[:].bitcast(F32)), in1=bc16(W_tm[:]), op=ALU.mult),
                 reads=["xc", "W_tm"], writes=["xw"])
            for g in range(2):
                P.op("tensor", R.matmul(PS(2 + g), lhsT=Btm[:, g, :], rhs=xw[:, g * 512:(g + 1) * 512],
                                                        start=True, stop=True),
                     reads=["Btm", "xw"], writes=psk(2 + g))
            s_ps = pst[:, 2:4, :].rearrange("p a b -> p (a b)")
            P.op("vector", R.tensor_tensor(out=v3(H[:]), in0=v3(H[:]), in1=bc16(cd_t[:]), op=ALU.mult),
                 reads=["H", "cd"], writes=["H"])
            P.op("vector", R.tensor_tensor(out=H[:], in0=H[:], in1=s_ps, op=ALU.add),
                 reads=["H"] + psk(2, 3), writes=["H"])
            if full:
                P.op("scalar", R.activation(out=Hr[:], in_=H[:], func=AF.Copy), reads=["H"], writes=["Hr"])

        def out_proj():
            for q in range(4):
                slot = load_panel(PAN_OUT + q)
                w = wv(slot, 16, 256)
                for mm in range(2):
                    m = q * 2 + mm
                    b = next_bank()
                    for k in range(16):
                        P.op("tensor", R.matmul(
                            PS(b, T), lhsT=w[:, k, mm * 128:(mm + 1) * 128], rhs=ycat[:, k, :],
                            start=(k == 0), stop=(k == 15)),
                             reads=[("ycat", k), ("w", slot)], writes=psk(b))
                    P.op("vector", R.tensor_tensor(out=x_t[:, m, :], in0=x_t[:, m, :], in1=PS(b, T),
                                                                       op=ALU.add),
                         reads=psk(b) + ["x"], writes=["x"])

        def mlp():
            for half in range(2):
                for q in range(4):
                    slot = load_panel(PAN_UP + half * 4 + q)
                    w = wv(slot, 8, 512)
                    for f in range(4):
                        fi = q * 4 + f
                        b = next_bank()
                        for k in range(8):
                            P.op("tensor", R.matmul(
                                PS(b, T), lhsT=w[:, k, f * 128:(f + 1) * 128], rhs=h_t[:, k, :],
                                start=(k == 0), stop=(k == 7)),
                                 reads=["h", ("w", slot)], writes=psk(b))
                        hk = "hid0" if fi < 8 else "hid1"
                        P.op("scalar", R.activation(out=relu_t[:], in_=PS(b, T), func=AF.Relu),
                             reads=psk(b), writes=["relu"])
                        P.op("vector", R.tensor_tensor(out=hid[:, fi, :], in0=relu_t[:], in1=relu_t[:],
                                                                        op=ALU.mult),
                             reads=["relu"], writes=[hk])
                for mg in range(2):
                    for kg in range(2):
                        slot = load_panel(PAN_DN + half * 4 + mg * 2 + kg)
                        w = wv(slot, 8, 512)
                        for m in range(4):
                            for k in range(8):
                                P.op("tensor", R.matmul(
                                    PS(4 + m, T), lhsT=w[:, k, m * 128:(m + 1) * 128], rhs=hid[:, kg * 8 + k, :],
                                    start=(kg == 0 and k == 0), stop=(kg == 1 and k == 7)),
                                     reads=["hid0" if kg == 0 else "hid1", ("w", slot)], writes=psk(4 + m))
                    for m in range(4):
                        mi = mg * 4 + m
                        P.op("vector", R.tensor_tensor(out=x_t[:, mi, :], in0=x_t[:, mi, :],
                                                                             in1=PS(4 + m, T), op=ALU.add),
                             reads=psk(4 + m) + ["x"], writes=["x"])

        P.dma("sync", xh_t[:], xh_d.rearrange("(c p) t -> p c t", p=128), writes=["xh"])
        rmsnorm(xh_t[:], "xh", HALO, nmix, sqh_t[:], "sqh", hh_t[:], "hh")
        xT_v = xT_d.rearrange("(c p) t -> p c t", p=128)
        if full:
            xo_v = xo_d.rearrange("(c p) t -> p c t", p=128)
            yo_v = yo_d.rearrange("(c p) t -> p c t", p=128)
        for it in range(NTILE):
            first = it == 0
            tsl = slice(it * T, (it + 1) * T)
            P.dma("sync", x_t[:], xT_v[:, :, tsl], writes=["x"])
            rmsnorm(x_t[:], "x", T, nmix, sq_t, "M", h_t[:], "h")
            dt_panel()
            if full:
                z_panel(0)
                z_panel(1)
            xbc_panel(0, first)
            xbc_panel(1, first)
            xbc_panel(2, first, chunks=(0, 1, 2, 3) if full else (0, 1))
            if full:
                for j in range(8):
                    conv_panel(j, first)
            for s in range(NSUB):
                ssd_chunk(s)
            if full:
                out_proj()
                rmsnorm(x_t[:], "x", T, nmlp, sq_t, "M", h_t[:], "h")
                mlp()
                finals.append(P.dma("sync", xo_v[:, :, tsl], x_t[:], reads=["x"]))
                rmsnorm(x_t[:], "x", T, fnw, sq_t, "M", yfin[:], "yfin")
                finals.append(P.dma("sync", yo_v[:, :, tsl], yfin[:], reads=["yfin"], writes=[("xsT", c) for c in range(8)]))
        if not full:
            finals.append(P.dma("sync", hloc_d, H[:], reads=["H"]))
            finals.append(P.dma("sync", dtot_d, dsum[:], reads=["dsum"]))
        P.finish(block, finals)
    return nc


def _consts():
    i = np.arange(128)
    tri = (i[:, None] <= i[None, :]).astype(np.float32)
    sup = (i[:, None] > i[None, :]).astype(np.float32)
    ones = np.ones((128, 128), np.float32)
    ident = np.eye(128, dtype=np.float32)
    return np.stack([tri, sup, ones, ident])


def _layer_arrays(inp, l):
    w_in = np.asarray(inp["w_in"][l], np.float32)
    W = w_in.reshape(8, 128, -1)
    wall = np.zeros((NPAN, 128, 4096), np.float32)

    def put(pn, cols, width):
        blk = W[:, :, cols].transpose(1, 0, 2)
        buf = np.zeros((128, 8, width), np.float32)
        buf[:, :, :blk.shape[2]] = blk
        wall[pn, :, :8 * width] = buf.reshape(128, -1)

    for j in range(8):
        cols = np.concatenate([np.arange(j * 128, (j + 1) * 128) + o for o in (0, 1024, 2048)])
        put(j, cols, 512)
    for q in range(3):
        put(8 + q, np.arange(4096 + q * 512, 4096 + (q + 1) * 512), 512)
    for zp in range(2):
        put(11 + zp, np.arange(3072 + zp * 512, 3072 + (zp + 1) * 512), 512)
    put(13, np.arange(5632, 5648), 512)
    wo = np.asarray(inp["w_out"][l], np.float32).reshape(16, 128, 1024)
    for q in range(4):
        wall[PAN_OUT + q] = wo[:, :, q * 256:(q + 1) * 256].transpose(1, 0, 2).reshape(128, -1)
    wu = np.asarray(inp["w_up"][l], np.float32).reshape(8, 128, 4096)
    for q in range(8):
        wall[PAN_UP + q] = wu[:, :, q * 512:(q + 1) * 512].transpose(1, 0, 2).reshape(128, -1)
    wd = np.asarray(inp["w_down"][l], np.float32)
    for half in range(2):
        for mg in range(2):
            for kg in range(2):
                r0 = half * 2048 + kg * 1024
                blk = wd[r0:r0 + 1024, mg * 512:(mg + 1) * 512].reshape(8, 128, 512)
                wall[PAN_DN + half * 4 + mg * 2 + kg] = blk.transpose(1, 0, 2).reshape(128, -1)

    def fm(v):
        return np.asarray(v, np.float32).reshape(-1, 128).T

    def rep(v):
        return np.tile(np.asarray(v, np.float32)[None, :], (128, 1))

    scw = np.asarray(inp["short_conv_w"][l], np.float32)
    ccw = np.asarray(inp["ssd_conv_w"][l], np.float32)
    pv = np.concatenate([
        fm(inp["norm_mix_w"][l]), fm(inp["norm_mlp_w"][l]), fm(inp["final_norm_w"]),
        scw.reshape(3, 8, 128).transpose(2, 1, 0).reshape(128, 24),
        ccw.reshape(4, 12, 128).transpose(2, 1, 0).reshape(128, 48),
        fm(inp["ssd_conv_b"][l]),
        rep(inp["dt_bias"][l]), rep(inp["a_log"][l]), rep(inp["d_skip"][l]),
    ], axis=1).astype(np.float32)
    assert pv.shape == (128, 156)
    snw = np.tile(np.asarray(inp["ssd_norm_w"][l], np.float32)[None, :], (128, 1))
    return {"wall": wall, "pvec": np.ascontiguousarray(pv), "snw": np.ascontiguousarray(snw)}


_PROGS = {}


def _prog(mode):
    if mode not in _PROGS:
        _PROGS[mode] = build_program(mode)
    return _PROGS[mode]


def kernel(**inputs):
    x = np.asarray(inputs["x"], np.float32)
    cst = _consts()
    xT = []
    for c in range(8):
        b, k = divmod(c, 4)
        xT.append(np.ascontiguousarray(x[b, k * NT:(k + 1) * NT, :].T))
    yT = None
    for l in range(DEPTH):
        la = _layer_arrays(inputs, l)
        xh = []
        for c in range(8):
            b, k = divmod(c, 4)
            if k == 0:
                xh.append(np.zeros((D, HALO), np.float32))
            else:
                xh.append(np.ascontiguousarray(xT[c - 1][:, NT - HALO:]))
        base = [{"xT": xT[c], "xh": xh[c], "wall": la["wall"], "pvec": la["pvec"], "snw": la["snw"], "cst": cst}
                for c in range(8)]
        resA = run_bass_kernel_spmd(_prog("A"), base, core_ids=list(range(8))).results
        in_b = []
        for c in range(8):
            b, k = divmod(c, 4)
            hp = np.zeros((3, 128, 1024), np.float32)
            dp = np.zeros((3, 128, 16), np.float32)
            for j in range(3):
                if k - 1 - j >= 0:
                    hp[j] = resA[c - 1 - j]["hloc"]
                    dp[j] = resA[c - 1 - j]["dtot"]
            m = dict(base[c])
            m["hp"] = hp
            m["dp"] = dp
            in_b.append(m)
        resB = run_bass_kernel_spmd(_prog("B"), in_b, core_ids=list(range(8))).results
        xT = [np.ascontiguousarray(resB[c]["xo"]) for c in range(8)]
        yT = [resB[c]["yo"] for c in range(8)]
    out = np.empty((2, 4 * NT, D), np.float32)
    for c in range(8):
        b, k = divmod(c, 4)
        out[b, k * NT:(k + 1) * NT, :] = yT[c].T
    return out
```

```python
import numpy as np
from contextlib import ExitStack
import concourse.bass as bass
import concourse.mybir as mybir
from concourse.bass_utils import run_bass_kernel_spmd

F32 = mybir.dt.float32
F32R = mybir.dt.float32r
AF = mybir.ActivationFunctionType
ALU = mybir.AluOpType

DEPTH = 4
D = 1024
NT = 2048
T = 256
NSUB = T // 128
NTILE = NT // T
HALO = 4
EPS = 1e-5
NPAN = 34
PAN_OUT = 14
PAN_UP = 18
PAN_DN = 26
NWBUF = 2


class _Rec:
    def __getattr__(self, name):
        def f(*a, **kw):
            return (name, a, kw)
        return f


R = _Rec()


class Prog:
    ENG = ["tensor", "vector", "scalar", "gpsimd", "sync"]

    def __init__(self, nc, st, ndma=16):
        self.nc = nc
        self.lists = {e: [] for e in self.ENG}
        self.count = {e: 0 for e in self.ENG}
        self.sem = {}
        self.seen = {e: {} for e in self.ENG}
        self.last_w = {}
        self.reads = {}
        self.last_acc = {}
        self.dma_cnt = []
        self.dma_rr = 0
        for e in self.ENG:
            self.sem[e] = st.enter_context(nc.semaphore("s_" + e))
        for j in range(ndma):
            self.sem[("dma", j)] = st.enter_context(nc.semaphore("dq%d" % j))
            self.dma_cnt.append(0)

    def token_wait(self, eng, tok, force=False):
        key, val = tok
        if key == eng and eng == "tensor" and not force:
            return
        cur = self.seen[eng].get(key, 0)
        if cur >= val:
            return
        self.seen[eng][key] = val
        sem = self.sem[key]
        self.lists[eng].append(lambda e, sem=sem, val=val: e.wait_ge(sem, val))

    def _deps(self, eng, reads, writes, extra, newgroup=False):
        deps = list(extra)
        if newgroup and eng == "tensor":
            for k in writes:
                lw = self.last_w.get(k)
                if lw is not None and lw[0] == "tensor" and not self.reads.get(k):
                    self.token_wait(eng, lw, force=True)
        for k in reads:
            if k in self.last_w:
                deps.append(self.last_w[k])
        for k in writes:
            if k in self.last_w:
                deps.append(self.last_w[k])
            deps.extend(self.reads.get(k, ()))
        for k in list(reads) + list(writes):
            if isinstance(k, tuple) and k[0] == "ps":
                for oe, t in self.last_acc.get(k, {}).items():
                    if oe != eng:
                        deps.append(t)
        for t in deps:
            self.token_wait(eng, t)

    def _commit(self, tok, reads, writes):
        for k in list(reads) + list(writes):
            if isinstance(k, tuple) and k[0] == "ps":
                self.last_acc.setdefault(k, {})[tok[0]] = tok
        for k in writes:
            self.last_w[k] = tok
            self.reads[k] = []
        for k in reads:
            if k not in writes:
                self.reads.setdefault(k, []).append(tok)

    def op(self, eng, fn, reads=(), writes=(), extra=(), newgroup=False):
        self._deps(eng, reads, writes, extra, newgroup)
        self.count[eng] += 1
        idx = self.count[eng]
        sem = self.sem[eng]
        name, a, kw = fn
        self.lists[eng].append(lambda e, name=name, a=a, kw=kw, sem=sem: getattr(e, name)(*a, **kw).then_inc(sem, 1))
        tok = (eng, idx)
        self._commit(tok, reads, writes)
        return tok

    def dma(self, eng, out, in_, reads=(), writes=(), extra=()):
        self._deps(eng, reads, writes, extra)
        j = self.dma_rr % len(self.dma_cnt)
        self.dma_rr += 1
        key = ("dma", j)
        if self.dma_cnt[j] > 0:
            self.token_wait(eng, (key, self.dma_cnt[j]))
        self.dma_cnt[j] += 16
        val = self.dma_cnt[j]
        sem = self.sem[key]
        self.lists[eng].append(
            lambda e, sem=sem, out=out, in_=in_: e.dma_start(out=out, in_=in_).then_inc(sem, 16))
        tok = (key, val)
        self._commit(tok, reads, writes)
        return tok

    def finish(self, block, final_tokens):
        for t in final_tokens:
            self.token_wait("sync", t)
        for en in self.ENG:
            def body(e, en=en):
                for f in self.lists[en]:
                    f(e)
            getattr(block, en)(body)


def build_program(mode):
    full = mode == "B"
    nc = bass.Bass("TRN2", target_bir_lowering=False)

    def din(name, shape):
        return nc.dram_tensor(name, list(shape), F32, kind="ExternalInput").ap()

    def dout(name, shape):
        return nc.dram_tensor(name, list(shape), F32, kind="ExternalOutput").ap()

    xT_d = din("xT", [D, NT])
    xh_d = din("xh", [D, HALO])
    wall_d = din("wall", [NPAN, 128, 4096])
    pvec_d = din("pvec", [128, 8 * 3 + 8 * 3 + 12 * 4 + 12 + 16 * 3])
    snw_d = din("snw", [128, 1024])
    cst_d = din("cst", [4, 128, 128])
    if full:
        hp_d = din("hp", [3, 128, 1024])
        dp_d = din("dp", [3, 128, 16])
        xo_d = dout("xo", [D, NT])
        yo_d = dout("yo", [D, NT])
    else:
        hloc_d = dout("hloc", [128, 1024])
        dtot_d = dout("dtot", [128, 16])

    with ExitStack() as st:
        E = st.enter_context

        def sb(name, shape, dt=F32):
            return E(nc.sbuf_tensor("sb_" + name, list(shape), dt))

        x_t = sb("x_t", [128, 8, T])
        xh_t = sb("xh_t", [128, 8, HALO])
        h_t = sb("h_t", [128, 8, T], F32R)
        hh_t = sb("hh_t", [128, 8, HALO], F32R)
        sqh_t = sb("sqh_t", [128, 8, HALO], F32R)
        rs_t = sb("rs_t", [128, T])
        ln_t = sb("ln_t", [128, T])
        vbuf = sb("vbuf", [128, 8, HALO + T])
        uc_sb = sb("uc_sb", [128, T])
        acc_a = sb("acc_a", [128, T])
        acc_b = sb("acc_b", [128, T])
        ycat = sb("ycat", [128, 16, T], F32R)
        xpre = sb("xpre", [128, 12, HALO + T])
        xsT = sb("xsT", [128, 8, T])
        bcT = sb("bcT", [128, 4, T], F32R)
        sz = sb("sz", [128, NSUB, 1024])
        dt_tm = sb("dt_tm", [128, NSUB, 16])
        dt1 = sb("dt1", [128, 16])
        a_tm = sb("a_tm", [128, 16])
        acum_sb = sb("acum_sb", [128, 16])
        E_tm = sb("E_tm", [128, 16])
        W_tm = sb("W_tm", [128, 16])
        cd_t = sb("cd_t", [128, 16])
        scr = sb("scr", [128, 4096])
        amask = scr[:, 0:2048]
        Dk = scr[:, 2048:4096]
        hid = sb("hid", [128, 16, T], F32R)
        M_t = sb("M_t", [128, 2048], F32R)
        sq_t = M_t[:].rearrange("p (c t) -> p c t", t=T)
        Gm = sb("Gm", [128, 2, 128])
        xc = sb("xc", [128, 1024], F32R)
        xsD = sb("xsD", [128, 1024], F32R)
        xw = sb("xw", [128, 1024], F32R)
        Btm = sb("Btm", [128, 2, 128], F32R)
        t1 = sb("t1", [128, 1024])
        t2 = sb("t2", [128, 1024])
        gn = sb("gn", [128, 1024])
        junk = sb("junk", [128, 512])
        ssq = sb("ssq", [128, 2])
        rsg = sb("rsg", [128, 2])
        H = sb("H", [128, 1024])
        Hr = sb("Hr", [128, 1024], F32R)
        relu_t = sb("relu_t", [128, T])
        wbuf = sb("wbuf", [128, NWBUF, 4096], F32R)
        cst = sb("cst", [128, 4, 128])
        cst_r = sb("cst_r", [128, 4, 128], F32R)
        pvec = sb("pvec", [128, 156])
        snw = sb("snw", [128, 1024])
        A_bc = sb("A_bc", [128, 16])
        dsum = sb("dsum", [128, 16])
        epsb = sb("epsb", [128, 1])
        oneb = sb("oneb", [128, 1])
        if full:
            dp_t = sb("dp_t", [128, 3, 16])
            yfin = xsT
        pst = E(nc.psum_tensor("pst", [128, 8, 512], F32))

        P = Prog(nc, st)
        block = E(nc.Block())

        tri = cst[:, 0, :]
        sup = cst[:, 1, :]
        ident = cst[:, 3, :]
        ones_r = cst_r[:, 2, :]
        ident_r = cst_r[:, 3, :]
        nmix = pvec[:, 0:8]
        nmlp = pvec[:, 8:16]
        fnw = pvec[:, 16:24]
        scw = pvec[:, 24:48].rearrange("p (c k) -> p c k", k=3)
        ccw = pvec[:, 48:96].rearrange("p (c k) -> p c k", k=4)
        ccb = pvec[:, 96:108]
        dtb = pvec[:, 108:124]
        alog = pvec[:, 124:140]
        dsk = pvec[:, 140:156]

        def PS(b, n=512, p0=0, p1=128, c0=0):
            return pst[p0:p1, b, c0:c0 + n]

        def psk(*bs):
            return [("ps", b) for b in bs]

        P.dma("sync", cst[:], cst_d.rearrange("c p q -> p c q"), writes=["cst"])
        P.dma("gpsimd", cst_r[:], cst_d.rearrange("c p q -> p c q"), writes=["cst_r"])
        P.dma("sync", pvec[:], pvec_d, writes=["pvec"])
        P.dma("sync", snw[:], snw_d, writes=["snw"])
        P.op("vector", R.memset(epsb[:], EPS), writes=["epsb"])
        P.op("vector", R.memset(oneb[:], 1.0), writes=["oneb"])
        P.op("scalar", R.activation(out=A_bc[:], in_=alog, func=AF.Exp), reads=["pvec"], writes=["A_bc"])
        P.op("vector", R.tensor_scalar(out=A_bc[:], in0=A_bc[:], scalar1=-1.0, scalar2=None, op0=ALU.mult),
             reads=["A_bc"], writes=["A_bc"])
        if full:
            hp_l = [t1, t2, gn]
            for j in range(3):
                P.dma("sync", hp_l[j][:], hp_d[j], writes=[["t1", "t2", "gn"][j]])
            P.dma("sync", dp_t[:], dp_d.rearrange("j p q -> p j q"), writes=["dp"])
            P.op("scalar", R.activation(out=dp_t[:], in_=dp_t[:], func=AF.Exp), reads=["dp"], writes=["dp"])
            def bc16(ap):
                return ap.unsqueeze(2).to_broadcast([128, 16, 64])
            def v3(ap):
                return ap.rearrange("p (h q) -> p h q", q=64)
            P.op("vector", R.tensor_tensor(out=v3(H[:]), in0=v3(hp_l[2][:]), in1=bc16(dp_t[:, 1, :]), op=ALU.mult),
                 reads=["gn", "dp"], writes=["H"])
            P.op("vector", R.tensor_tensor(out=H[:], in0=H[:], in1=hp_l[1][:], op=ALU.add),
                 reads=["t2", "H"], writes=["H"])
            P.op("vector", R.tensor_tensor(out=v3(H[:]), in0=v3(H[:]), in1=bc16(dp_t[:, 0, :]), op=ALU.mult),
                 reads=["dp", "H"], writes=["H"])
            P.op("vector", R.tensor_tensor(out=H[:], in0=H[:], in1=hp_l[0][:], op=ALU.add),
                 reads=["t1", "H"], writes=["H"])
        else:
            P.op("vector", R.memset(H[:], 0.0), writes=["H"])
            P.op("vector", R.memset(dsum[:], 0.0), writes=["dsum"])
        P.op("scalar", R.activation(out=Hr[:], in_=H[:], func=AF.Copy), reads=["H"], writes=["Hr"])

        def bc16(ap):
            return ap.unsqueeze(2).to_broadcast([128, 16, 64])

        def v3(ap):
            return ap.rearrange("p (h q) -> p h q", q=64)

        wstate = {"n": 0}

        def load_panel(pn, nfl=4096):
            slot = wstate["n"] % NWBUF
            wstate["n"] += 1
            P.dma("gpsimd", wbuf[:, slot, 0:nfl], wall_d[pn][:, 0:nfl], writes=[("w", slot)])
            return slot

        def wv(slot, kchunks, width):
            return wbuf[:, slot, 0:kchunks * width].rearrange("p (k c) -> p k c", c=width)

        psrr = {"n": 0}

        def next_bank():
            b = psrr["n"] % 4
            psrr["n"] += 1
            return b

        def rmsnorm(x_ap, xkey, n, nw, sq_ap, sqkey, out_ap, outkey, out_dt_r=True):
            P.op("scalar", R.activation(out=sq_ap, in_=x_ap, func=AF.Square),
                 reads=[xkey], writes=[sqkey])
            b = next_bank()
            for c in range(8):
                P.op("tensor", R.matmul(PS(b, n), lhsT=ones_r, rhs=sq_ap[:, c, :],
                                                        start=(c == 0), stop=(c == 7)),
                     reads=[sqkey, "cst_r"], writes=psk(b))
            P.op("scalar", R.activation(out=ln_t[:, 0:n], in_=PS(b, n), func=AF.Ln,
                                                 bias=epsb[:], scale=1.0 / D),
                 reads=psk(b) + ["epsb"], writes=["ln"])
            P.op("scalar", R.activation(out=rs_t[:, 0:n], in_=ln_t[:, 0:n], func=AF.Exp, scale=-0.5),
                 reads=["ln"], writes=["rs"])
            for c in range(8):
                P.op("vector", R.scalar_tensor_tensor(
                    out=out_ap[:, c, :], in0=x_ap[:, c, :], scalar=nw[:, c:c + 1], in1=rs_t[:, 0:n],
                    op0=ALU.mult, op1=ALU.mult),
                     reads=[xkey, "rs", "pvec"], writes=[outkey])

        def fm_chunk(slot, mc, h_ap, hkey, n, width=512):
            b = next_bank()
            w = wv(slot, 8, width)
            for k in range(8):
                P.op("tensor", R.matmul(PS(b, n), lhsT=w[:, k, mc * 128:(mc + 1) * 128],
                                                        rhs=h_ap[:, k, :], start=(k == 0), stop=(k == 7)),
                     reads=[hkey, ("w", slot)], writes=psk(b))
            return b

        def conv_panel(j, first):
            slot = load_panel(j, 3072)
            if first:
                b1 = fm_chunk(slot, 1, hh_t, "hh", HALO, width=384)
                P.op("scalar", R.activation(out=uc_sb[:, 0:HALO], in_=PS(b1, HALO), func=AF.Copy),
                     reads=psk(b1), writes=["uc"])
                b2 = fm_chunk(slot, 2, hh_t, "hh", HALO, width=384)
                P.op("vector", R.tensor_tensor(out=vbuf[:, j, 0:HALO], in0=uc_sb[:, 0:HALO],
                                                         in1=PS(b2, HALO), op=ALU.mult),
                     reads=psk(b2) + ["uc"], writes=[("v", j)])
            b1 = fm_chunk(slot, 1, h_t, "h", T, width=384)
            P.op("scalar", R.activation(out=uc_sb[:], in_=PS(b1, T), func=AF.Copy),
                 reads=psk(b1), writes=["uc"])
            b2 = fm_chunk(slot, 2, h_t, "h", T, width=384)
            P.op("vector", R.tensor_tensor(out=vbuf[:, j, HALO:HALO + T], in0=uc_sb[:],
                                                     in1=PS(b2, T), op=ALU.mult),
                 reads=psk(b2) + ["uc"], writes=[("v", j)])
            b0 = fm_chunk(slot, 0, h_t, "h", T, width=384)
            P.op("vector", R.tensor_scalar(out=acc_a[:], in0=vbuf[:, j, HALO:HALO + T],
                                                     scalar1=scw[:, j, 2:3], scalar2=None, op0=ALU.mult),
                 reads=[("v", j), "pvec"], writes=["acc_a"])
            for kk in (1, 0):
                sh = 2 - kk
                P.op("vector", R.scalar_tensor_tensor(
                    out=acc_a[:], in0=vbuf[:, j, HALO - sh:HALO - sh + T], scalar=scw[:, j, kk:kk + 1],
                    in1=acc_a[:], op0=ALU.mult, op1=ALU.add),
                     reads=[("v", j), "pvec", "acc_a"], writes=["acc_a"])
            P.op("vector", R.tensor_tensor(out=ycat[:, j, :], in0=acc_a[:], in1=PS(b0, T), op=ALU.mult),
                 reads=psk(b0) + ["acc_a"], writes=[("ycat", j)])
            P.op("scalar", R.activation(out=vbuf[:, j, 0:HALO], in_=vbuf[:, j, T:T + HALO], func=AF.Copy),
                 reads=[("v", j)], writes=[("v", j)])

        def xbc_panel(q, first, chunks=(0, 1, 2, 3)):
            slot = load_panel(8 + q)
            for mc in chunks:
                c = q * 4 + mc
                if first:
                    bh = fm_chunk(slot, mc, hh_t, "hh", HALO)
                    P.op("scalar", R.activation(out=xpre[:, c, 0:HALO], in_=PS(bh, HALO),
                                                                       func=AF.Copy),
                         reads=psk(bh), writes=[("xp", c)])
                b = fm_chunk(slot, mc, h_t, "h", T)
                P.op("scalar", R.activation(out=xpre[:, c, HALO:HALO + T], in_=PS(b, T),
                                                                 func=AF.Copy),
                     reads=psk(b), writes=[("xp", c)])
                P.op("vector", R.tensor_scalar(
                    out=acc_b[:], in0=xpre[:, c, HALO:HALO + T], scalar1=ccw[:, c, 3:4], scalar2=ccb[:, c:c + 1],
                    op0=ALU.mult, op1=ALU.add),
                     reads=[("xp", c), "pvec"], writes=["acc_b"])
                for kk in (2, 1, 0):
                    sh = 3 - kk
                    P.op("vector", R.scalar_tensor_tensor(
                        out=acc_b[:], in0=xpre[:, c, HALO - sh:HALO - sh + T], scalar=ccw[:, c, kk:kk + 1],
                        in1=acc_b[:], op0=ALU.mult, op1=ALU.add),
                         reads=[("xp", c), "pvec", "acc_b"], writes=["acc_b"])
                if c < 8:
                    P.op("scalar", R.activation(out=xsT[:, c, :], in_=acc_b[:], func=AF.Silu),
                         reads=["acc_b"], writes=[("xsT", c)])
                else:
                    P.op("scalar", R.activation(out=bcT[:, c - 8, :], in_=acc_b[:], func=AF.Silu),
                         reads=["acc_b"], writes=[("bcT", c - 8)])
                P.op("scalar", R.activation(out=xpre[:, c, 0:HALO], in_=xpre[:, c, T:T + HALO],
                                                           func=AF.Copy),
                     reads=[("xp", c)], writes=[("xp", c)])

        def z_panel(zp):
            slot = load_panel(11 + zp)
            w = wv(slot, 8, 512)
            for s in range(NSUB):
                b = next_bank()
                for k in range(8):
                    P.op("tensor", R.matmul(PS(b), lhsT=h_t[:, k, s * 128:(s + 1) * 128],
                                                                 rhs=w[:, k, :], start=(k == 0), stop=(k == 7)),
                         reads=["h", ("w", slot)], writes=psk(b))
                P.op("scalar", R.activation(out=sz[:, s, zp * 512:(zp + 1) * 512], in_=PS(b),
                                                                 func=AF.Silu),
                     reads=psk(b), writes=[("sz", s)])

        def dt_panel():
            slot = load_panel(13, 128)
            w = wv(slot, 8, 16)
            for s in range(NSUB):
                b = next_bank()
                for k in range(8):
                    P.op("tensor", R.matmul(PS(b, 16), lhsT=h_t[:, k, s * 128:(s + 1) * 128],
                                                                 rhs=w[:, k, 0:16], start=(k == 0), stop=(k == 7)),
                         reads=["h", ("w", slot)], writes=psk(b))
                P.op("vector", R.tensor_tensor(out=dt1[:], in0=PS(b, 16), in1=dtb, op=ALU.add),
                     reads=psk(b) + ["pvec"], writes=["dt1"])
                P.op("scalar", R.activation(out=dt1[:], in_=dt1[:], func=AF.Exp),
                     reads=["dt1"], writes=["dt1"])
                P.op("scalar", R.activation(out=dt_tm[:, s, :], in_=dt1[:], func=AF.Ln,
                                                           bias=oneb[:], scale=1.0),
                     reads=["dt1", "oneb"], writes=[("dt", s)])

        finals = []
        dbgstate = {"n": 0}
        def ssd_chunk(s):
            tok = slice(s * 128, (s + 1) * 128)
            dts = dt_tm[:, s, :]
            P.op("vector", R.tensor_tensor(out=a_tm[:], in0=dts, in1=A_bc[:], op=ALU.mult),
                 reads=[("dt", s), "A_bc"], writes=["a_tm"])
            P.op("tensor", R.matmul(PS(6, 16), lhsT=tri, rhs=a_tm[:], start=True, stop=True),
                 reads=["cst", "a_tm"], writes=psk(6))
            P.op("tensor", R.matmul(PS(6, 16, c0=16), lhsT=cst[:, 2, :], rhs=a_tm[:], start=True, stop=True),
                 reads=["cst", "a_tm"], writes=psk(6), newgroup=True)
            P.op("scalar", R.activation(out=acum_sb[:], in_=PS(6, 16), func=AF.Copy),
                 reads=psk(6), writes=["acum"])
            P.op("scalar", R.activation(out=E_tm[:], in_=PS(6, 16), func=AF.Exp),
                 reads=psk(6), writes=["E_tm"])
            P.op("scalar", R.activation(out=cd_t[:], in_=PS(6, 16, c0=16), func=AF.Exp),
                 reads=psk(6), writes=["cd"])
            P.op("vector", R.tensor_tensor(out=W_tm[:], in0=PS(6, 16, c0=16), in1=acum_sb[:], op=ALU.subtract),
                 reads=psk(6) + ["acum"], writes=["W_tm"])
            P.op("scalar", R.activation(out=W_tm[:], in_=W_tm[:], func=AF.Exp),
                 reads=["W_tm"], writes=["W_tm"])
            if not full:
                P.op("vector", R.tensor_tensor(out=dsum[:], in0=dsum[:], in1=PS(6, 16, c0=16), op=ALU.add),
                     reads=psk(6) + ["dsum"], writes=["dsum"])
            if False:
                gi = dbgstate["n"]
                dbgstate["n"] += 1
                for (i, (ap, key)) in enumerate([(dts, ("dt", s)), (a_tm[:], "a_tm"), (acum_sb[:], "acum"), (W_tm[:], "W_tm"), (cd_t[:], "cd")]):
                    finals.append(P.dma("sync", dbg_d[gi, :, i * 16:(i + 1) * 16], ap, reads=[key]))
            for g in range(2):
                P.op("tensor", R.transpose(PS(7, 128, c0=256 + g * 128), bcT[:, g, tok].bitcast(F32), ident),
                     reads=[("bcT", g), "cst"], writes=psk(7))
            P.op("scalar", R.activation(out=Btm[:].rearrange("p g n -> p (g n)"), in_=PS(7, 256, c0=256),
                                                 func=AF.Copy),
                 reads=psk(7), writes=["Btm"])
            for c in range(8):
                bb = 4 + c // 4
                P.op("tensor", R.transpose(PS(bb, 128, c0=(c % 4) * 128), xsT[:, c, tok], ident),
                     reads=[("xsT", c), "cst"], writes=psk(bb))
            xs_ps = pst[:, 4:6, :].rearrange("p a b -> p (a b)")
            P.op("vector", R.tensor_tensor(out=v3(xc[:]), in0=v3(xs_ps), in1=bc16(dts), op=ALU.mult),
                 reads=psk(4, 5) + [("dt", s)], writes=["xc"])
            if full:
                P.op("vector", R.tensor_tensor(out=v3(xsD[:]), in0=v3(xs_ps), in1=bc16(dsk), op=ALU.mult),
                     reads=psk(4, 5) + ["pvec"], writes=["xsD"])
            if full:
                P.op("vector", R.tensor_tensor(
                    out=amask.rearrange("p (h l) -> p h l", l=128),
                    in0=a_tm[:].unsqueeze(2).to_broadcast([128, 16, 128]),
                    in1=tri.unsqueeze(1).to_broadcast([128, 16, 128]), op=ALU.mult),
                     reads=["a_tm", "cst"], writes=["amask"])
                for q in range(4):
                    P.op("tensor", R.matmul(PS(q), lhsT=sup, rhs=amask[:, q * 512:(q + 1) * 512],
                                                            start=True, stop=True),
                         reads=["amask", "cst"], writes=psk(q))
                for q in range(2):
                    P.op("scalar", R.activation(
                        out=Dk[:, q * 1024:(q + 1) * 1024],
                        in_=pst[:, 2 * q:2 * q + 2, :].rearrange("p a b -> p (a b)"), func=AF.Exp),
                         reads=psk(2 * q, 2 * q + 1), writes=["Dk"])
                for g in range(2):
                    P.op("tensor", R.matmul(PS(7, 128, c0=g * 128), lhsT=bcT[:, g, tok],
                                                            rhs=bcT[:, 2 + g, tok], start=True, stop=True),
                         reads=[("bcT", g), ("bcT", 2 + g)], writes=psk(7))
                P.op("vector", R.tensor_tensor(
                    out=Gm[:], in0=PS(7, 256).rearrange("p (g l) -> p g l", l=128),
                    in1=tri.unsqueeze(1).to_broadcast([128, 2, 128]), op=ALU.mult),
                     reads=psk(7) + ["cst"], writes=["Gm"])
                for g in range(2):
                    P.op("vector", R.tensor_tensor(
                        out=M_t[:, g * 1024:(g + 1) * 1024].rearrange("p (h l) -> p h l", l=128),
                        in0=Dk[:, g * 1024:(g + 1) * 1024].rearrange("p (h l) -> p h l", l=128),
                        in1=Gm[:, g, :].unsqueeze(1).to_broadcast([128, 8, 128]), op=ALU.mult),
                         reads=["Dk", "Gm"], writes=["M"])
                for g in range(2):
                    bb = 4 + g
                    P.op("tensor", R.matmul(PS(bb), lhsT=ident_r, rhs=xsD[:, g * 512:(g + 1) * 512],
                                                                   start=True, stop=False, skip_group_check=True),
                         reads=["xsD", "cst_r"], writes=psk(bb))
                    for hh in range(8):
                        hd = g * 8 + hh
                        P.op("tensor", R.matmul(
                            PS(bb, 64, c0=hh * 64), lhsT=M_t[:, hd * 128:(hd + 1) * 128],
                            rhs=xc[:, hd * 64:(hd + 1) * 64], start=False, stop=(hh == 7), skip_group_check=True),
                             reads=["M", "xc"], writes=psk(bb))
                for g in range(2):
                    P.op("tensor", R.matmul(PS(6 + g), lhsT=bcT[:, 2 + g, tok],
                                                            rhs=Hr[:, g * 512:(g + 1) * 512], start=True, stop=True),
                         reads=[("bcT", 2 + g), "Hr"], writes=psk(6 + g))
                yo_ps = pst[:, 6:8, :].rearrange("p a b -> p (a b)")
                y_ps = pst[:, 4:6, :].rearrange("p a b -> p (a b)")
                P.op("vector", R.tensor_tensor(out=v3(t1[:]), in0=v3(yo_ps), in1=bc16(E_tm[:]), op=ALU.mult),
                     reads=psk(6, 7) + ["E_tm"], writes=["t1"])
                P.op("vector", R.tensor_tensor(out=t2[:], in0=y_ps, in1=t1[:], op=ALU.add),
                     reads=psk(4, 5) + ["t1"], writes=["t2"])
                P.op("vector", R.tensor_tensor(out=t2[:], in0=t2[:], in1=sz[:, s, :], op=ALU.mult),
                     reads=["t2", ("sz", s)], writes=["t2"])
                for g in range(2):
                    P.op("scalar", R.activation(out=junk[:], in_=t2[:, g * 512:(g + 1) * 512],
                                                               func=AF.Square, accum_out=ssq[:, g:g + 1]),
                         reads=["t2"], writes=["junk", ("ssq", g)])
                P.op("scalar", R.activation(out=rsg[:], in_=ssq[:], func=AF.Ln, bias=epsb[:], scale=1.0 / 512),
                     reads=[("ssq", 0), ("ssq", 1), "epsb"], writes=["rsg"])
                P.op("scalar", R.activation(out=rsg[:], in_=rsg[:], func=AF.Exp, scale=-0.5),
                     reads=["rsg"], writes=["rsg"])
                for g in range(2):
                    P.op("vector", R.scalar_tensor_tensor(
                        out=gn[:, g * 512:(g + 1) * 512], in0=t2[:, g * 512:(g + 1) * 512], scalar=rsg[:, g:g + 1],
                        in1=snw[:, g * 512:(g + 1) * 512], op0=ALU.mult, op1=ALU.mult),
                         reads=["t2", "rsg", "snw"], writes=["gn"])
                for c in range(8):
                    bb = c // 4
                    P.op("tensor", R.transpose(PS(bb, 128, c0=(c % 4) * 128),
                                                                     gn[:, c * 128:(c + 1) * 128], ident),
                         reads=["gn", "cst"], writes=psk(bb))
                for bb in range(2):
                    P.op("scalar", R.activation(
                        out=ycat[:, 8 + bb * 4:12 + bb * 4, tok],
                        in_=PS(bb).rearrange("p (c t) -> p c t", t=128), func=AF.Copy),
                         reads=psk(bb), writes=[("ycat", 8 + bb * 4 + i) for i in range(4)])
            P.op("vector", R.tensor_tensor(out=v3(xw[:]), in0=v3(xc[:].bitcast(F32)), in1=bc16(W_tm[:]), op=ALU.mult),
                 reads=["xc", "W_tm"], writes=["xw"])
            for g in range(2):
                P.op("tensor", R.matmul(PS(2 + g), lhsT=Btm[:, g, :], rhs=xw[:, g * 512:(g + 1) * 512],
                                                        start=True, stop=True),
                     reads=["Btm", "xw"], writes=psk(2 + g))
            s_ps = pst[:, 2:4, :].rearrange("p a b -> p (a b)")
            P.op("vector", R.tensor_tensor(out=v3(H[:]), in0=v3(H[:]), in1=bc16(cd_t[:]), op=ALU.mult),
                 reads=["H", "cd"], writes=["H"])
            P.op("vector", R.tensor_tensor(out=H[:], in0=H[:], in1=s_ps, op=ALU.add),
                 reads=["H"] + psk(2, 3), writes=["H"])
            if full:
                P.op("scalar", R.activation(out=Hr[:], in_=H[:], func=AF.Copy), reads=["H"], writes=["Hr"])

        def out_proj():
            for q in range(4):
                slot = load_panel(PAN_OUT + q)
                w = wv(slot, 16, 256)
                for mm in range(2):
                    m = q * 2 + mm
                    b = next_bank()
                    for k in range(16):
                        P.op("tensor", R.matmul(
                            PS(b, T), lhsT=w[:, k, mm * 128:(mm + 1) * 128], rhs=ycat[:, k, :],
                            start=(k == 0), stop=(k == 15)),
                             reads=[("ycat", k), ("w", slot)], writes=psk(b))
                    P.op("vector", R.tensor_tensor(out=x_t[:, m, :], in0=x_t[:, m, :], in1=PS(b, T),
                                                                       op=ALU.add),
                         reads=psk(b) + ["x"], writes=["x"])

        def mlp():
            for half in range(2):
                for q in range(4):
                    slot = load_panel(PAN_UP + half * 4 + q)
                    w = wv(slot, 8, 512)
                    for f in range(4):
                        fi = q * 4 + f
                        b = next_bank()
                        for k in range(8):
                            P.op("tensor", R.matmul(
                                PS(b, T), lhsT=w[:, k, f * 128:(f + 1) * 128], rhs=h_t[:, k, :],
                                start=(k == 0), stop=(k == 7)),
                                 reads=["h", ("w", slot)], writes=psk(b))
                        hk = "hid0" if fi < 8 else "hid1"
                        P.op("scalar", R.activation(out=relu_t[:], in_=PS(b, T), func=AF.Relu),
                             reads=psk(b), writes=["relu"])
                        P.op("vector", R.tensor_tensor(out=hid[:, fi, :], in0=relu_t[:], in1=relu_t[:],
                                                                        op=ALU.mult),
                             reads=["relu"], writes=[hk])
                for mg in range(2):
                    for kg in range(2):
                        slot = load_panel(PAN_DN + half * 4 + mg * 2 + kg)
                        w = wv(slot, 8, 512)
                        for m in range(4):
                            for k in range(8):
                                P.op("tensor", R.matmul(
                                    PS(4 + m, T), lhsT=w[:, k, m * 128:(m + 1) * 128], rhs=hid[:, kg * 8 + k, :],
                                    start=(kg == 0 and k == 0), stop=(kg == 1 and k == 7)),
                                     reads=["hid0" if kg == 0 else "hid1", ("w", slot)], writes=psk(4 + m))
                    for m in range(4):
                        mi = mg * 4 + m
                        P.op("vector", R.tensor_tensor(out=x_t[:, mi, :], in0=x_t[:, mi, :],
                                                                             in1=PS(4 + m, T), op=ALU.add),
                             reads=psk(4 + m) + ["x"], writes=["x"])

        P.dma("sync", xh_t[:], xh_d.rearrange("(c p) t -> p c t", p=128), writes=["xh"])
        rmsnorm(xh_t[:], "xh", HALO, nmix, sqh_t[:], "sqh", hh_t[:], "hh")
        xT_v = xT_d.rearrange("(c p) t -> p c t", p=128)
        if full:
            xo_v = xo_d.rearrange("(c p) t -> p c t", p=128)
            yo_v = yo_d.rearrange("(c p) t -> p c t", p=128)
        for it in range(NTILE):
            first = it == 0
            tsl = slice(it * T, (it + 1) * T)
            P.dma("sync", x_t[:], xT_v[:, :, tsl], writes=["x"])
            rmsnorm(x_t[:], "x", T, nmix, sq_t, "M", h_t[:], "h")
            dt_panel()
            if full:
                z_panel(0)
                z_panel(1)
            xbc_panel(0, first)
            xbc_panel(1, first)
            xbc_panel(2, first, chunks=(0, 1, 2, 3) if full else (0, 1))
            if full:
                for j in range(8):
                    conv_panel(j, first)
            for s in range(NSUB):
                ssd_chunk(s)
            if full:
                out_proj()
                rmsnorm(x_t[:], "x", T, nmlp, sq_t, "M", h_t[:], "h")
                mlp()
                finals.append(P.dma("sync", xo_v[:, :, tsl], x_t[:], reads=["x"]))
                rmsnorm(x_t[:], "x", T, fnw, sq_t, "M", yfin[:], "yfin")
                finals.append(P.dma("sync", yo_v[:, :, tsl], yfin[:], reads=["yfin"], writes=[("xsT", c) for c in range(8)]))
        if not full:
            finals.append(P.dma("sync", hloc_d, H[:], reads=["H"]))
            finals.append(P.dma("sync", dtot_d, dsum[:], reads=["dsum"]))
        P.finish(block, finals)
    return nc


def _consts():
    i = np.arange(128)
    tri = (i[:, None] <= i[None, :]).astype(np.float32)
    sup = (i[:, None] > i[None, :]).astype(np.float32)
    ones = np.ones((128, 128), np.float32)
    ident = np.eye(128, dtype=np.float32)
    return np.stack([tri, sup, ones, ident])


def _layer_arrays(inp, l):
    w_in = np.asarray(inp["w_in"][l], np.float32)
    W = w_in.reshape(8, 128, -1)
    wall = np.zeros((NPAN, 128, 4096), np.float32)

    def put(pn, cols, width):
        blk = W[:, :, cols].transpose(1, 0, 2)
        buf = np.zeros((128, 8, width), np.float32)
        buf[:, :, :blk.shape[2]] = blk
        wall[pn, :, :8 * width] = buf.reshape(128, -1)

    for j in range(8):
        cols = np.concatenate([np.arange(j * 128, (j + 1) * 128) + o for o in (0, 1024, 2048)])
        put(j, cols, 384)
    for q in range(3):
        put(8 + q, np.arange(4096 + q * 512, 4096 + (q + 1) * 512), 512)
    for zp in range(2):
        put(11 + zp, np.arange(3072 + zp * 512, 3072 + (zp + 1) * 512), 512)
    put(13, np.arange(5632, 5648), 16)
    wo = np.asarray(inp["w_out"][l], np.float32).reshape(16, 128, 1024)
    for q in range(4):
        wall[PAN_OUT + q] = wo[:, :, q * 256:(q + 1) * 256].transpose(1, 0, 2).reshape(128, -1)
    wu = np.asarray(inp["w_up"][l], np.float32).reshape(8, 128, 4096)
    for q in range(8):
        wall[PAN_UP + q] = wu[:, :, q * 512:(q + 1) * 512].transpose(1, 0, 2).reshape(128, -1)
    wd = np.asarray(inp["w_down"][l], np.float32)
    for half in range(2):
        for mg in range(2):
            for kg in range(2):
                r0 = half * 2048 + kg * 1024
                blk = wd[r0:r0 + 1024, mg * 512:(mg + 1) * 512].reshape(8, 128, 512)
                wall[PAN_DN + half * 4 + mg * 2 + kg] = blk.transpose(1, 0, 2).reshape(128, -1)

    def fm(v):
        return np.asarray(v, np.float32).reshape(-1, 128).T

    def rep(v):
        return np.tile(np.asarray(v, np.float32)[None, :], (128, 1))

    scw = np.asarray(inp["short_conv_w"][l], np.float32)
    ccw = np.asarray(inp["ssd_conv_w"][l], np.float32)
    pv = np.concatenate([
        fm(inp["norm_mix_w"][l]), fm(inp["norm_mlp_w"][l]), fm(inp["final_norm_w"]),
        scw.reshape(3, 8, 128).transpose(2, 1, 0).reshape(128, 24),
        ccw.reshape(4, 12, 128).transpose(2, 1, 0).reshape(128, 48),
        fm(inp["ssd_conv_b"][l]),
        rep(inp["dt_bias"][l]), rep(inp["a_log"][l]), rep(inp["d_skip"][l]),
    ], axis=1).astype(np.float32)
    assert pv.shape == (128, 156)
    snw = np.tile(np.asarray(inp["ssd_norm_w"][l], np.float32)[None, :], (128, 1))
    return {"wall": wall, "pvec": np.ascontiguousarray(pv), "snw": np.ascontiguousarray(snw)}


_PROGS = {}


def _prog(mode):
    if mode not in _PROGS:
        _PROGS[mode] = build_program(mode)
    return _PROGS[mode]


def kernel(**inputs):
    x = np.asarray(inputs["x"], np.float32)
    cst = _consts()
    xT = []
    for c in range(8):
        b, k = divmod(c, 4)
        xT.append(np.ascontiguousarray(x[b, k * NT:(k + 1) * NT, :].T))
    yT = None
    for l in range(DEPTH):
        la = _layer_arrays(inputs, l)
        xh = []
        for c in range(8):
            b, k = divmod(c, 4)
            if k == 0:
                xh.append(np.zeros((D, HALO), np.float32))
            else:
                xh.append(np.ascontiguousarray(xT[c - 1][:, NT - HALO:]))
        base = [{"xT": xT[c], "xh": xh[c], "wall": la["wall"], "pvec": la["pvec"], "snw": la["snw"], "cst": cst}
                for c in range(8)]
        resA = run_bass_kernel_spmd(_prog("A"), base, core_ids=list(range(8))).results
        in_b = []
        for c in range(8):
            b, k = divmod(c, 4)
            hp = np.zeros((3, 128, 1024), np.float32)
            dp = np.zeros((3, 128, 16), np.float32)
            for j in range(3):
                if k - 1 - j >= 0:
                    hp[j] = resA[c - 1 - j]["hloc"]
                    dp[j] = resA[c - 1 - j]["dtot"]
            m = dict(base[c])
            m["hp"] = hp
            m["dp"] = dp
            in_b.append(m)
        resB = run_bass_kernel_spmd(_prog("B"), in_b, core_ids=list(range(8))).results
        xT = [np.ascontiguousarray(resB[c]["xo"]) for c in range(8)]
        yT = [resB[c]["yo"] for c in range(8)]
    out = np.empty((2, 4 * NT, D), np.float32)
    for c in range(8):
        b, k = divmod(c, 4)
        out[b, k * NT:(k + 1) * NT, :] = yT[c].T
    return out
```
